# Optimizing a Trainium2 kernel written in Bass

```python
import jax, jax.numpy as jnp
from jax import lax
import numpy as np

D_MODEL = 4096
BATCH = 2
SEQ = 4096
DEPTH = 4

GRID_W = 64
CTX_LEN = 256
EPS = 1e-6

MLA_HEADS = D_MODEL // 256
MLA_NOPE = 128
MLA_ROPE = 64
MLA_QK = MLA_NOPE + MLA_ROPE
MLA_V = 128
MLA_OUT = MLA_HEADS * MLA_V
Q_LORA = D_MODEL // 4
KV_LORA = D_MODEL // 8
ROPE_BASE = 10000.0
Q_BLOCK = 128

DN_HEADS = D_MODEL // 512
DN_DH = 128
DN_W = DN_HEADS * DN_DH
DN_CONV = 5
DN_CHUNK = 64

FN_GROUPS = 4
FN_GW = D_MODEL // 16
FN_W = FN_GROUPS * FN_GW

N_BRANCH = 3
GATE_RANK = D_MODEL // 16

D_FF = D_MODEL * 5 // 4
FFN_CONV = 3

STATE_SPLITS = (KV_LORA, KV_LORA + MLA_ROPE, KV_LORA + MLA_ROPE + 3 * DN_W, KV_LORA + MLA_ROPE + 3 * DN_W + 2 * DN_HEADS)
STATE_COLS = STATE_SPLITS[-1] + 2 * DN_HEADS
IN_SPLITS = STATE_SPLITS + (STATE_COLS, STATE_COLS + Q_LORA, STATE_COLS + Q_LORA + DN_W, STATE_COLS + Q_LORA + DN_W + FN_W)
IN_COLS = IN_SPLITS[-1] + GATE_RANK

kernel_name = "hybrid_mla_deltanet_fourier_dit"


def rms_norm(x, g):
    xf = x.astype(jnp.float32)
    y = xf * lax.rsqrt(jnp.mean(xf * xf, axis=-1, keepdims=True) + EPS)
    return (y * g.astype(jnp.float32)).astype(x.dtype)


def l2_norm(x):
    xf = x.astype(jnp.float32)
    return xf * lax.rsqrt(jnp.sum(xf * xf, axis=-1, keepdims=True) + EPS)


def modulate(h, shift, scale):
    return h * (1 + scale) + shift


def dwconv(x, w):
    k, ch = w.shape
    return lax.conv_general_dilated(x, w[:, None, :].astype(x.dtype), window_strides=(1,), padding=[(k // 2, k // 2)],
                                    dimension_numbers=('NWC', 'WIO', 'NWC'), feature_group_count=ch)


def axial_rope_tables(n_tokens, dtype):
    rows = n_tokens // GRID_W
    r = jnp.repeat(jnp.arange(rows, dtype=jnp.float32), GRID_W)
    col = jnp.tile(jnp.arange(GRID_W, dtype=jnp.float32), rows)
    half = MLA_ROPE // 2
    inv = ROPE_BASE ** (-jnp.arange(0, half, 2, dtype=jnp.float32) / half)
    ar = r[:, None] * inv
    ac = col[:, None] * inv
    return tuple(t.astype(dtype) for t in (jnp.cos(ar), jnp.sin(ar), jnp.cos(ac), jnp.sin(ac)))


def _rot(x, cos, sin):
    x1, x2 = jnp.split(x, 2, axis=-1)
    return jnp.concatenate([x1 * cos - x2 * sin, x2 * cos + x1 * sin], axis=-1)


def axial_rope(x, tabs):
    cr, sr, cc, sc = tabs
    xr, xc = jnp.split(x, 2, axis=-1)
    return jnp.concatenate([_rot(xr, cr, sr), _rot(xc, cc, sc)], axis=-1)


def mla_queries(cq, q_norm_g, w_uq, tabs):
    b, l, _ = cq.shape
    q = (rms_norm(cq, q_norm_g) @ w_uq).reshape(b, l, MLA_HEADS, MLA_QK)
    if tabs is None:
        return q
    q_rope = axial_rope(q[..., MLA_NOPE:], tuple(t[:, None, :] for t in tabs))
    return jnp.concatenate([q[..., :MLA_NOPE], q_rope], axis=-1)


def mla_keys_values(ckv, kr, kv_norm_g, w_ukv, tabs):
    b, l, _ = ckv.shape
    kv = (rms_norm(ckv, kv_norm_g) @ w_ukv).reshape(b, l, MLA_HEADS, MLA_NOPE + MLA_V)
    if tabs is not None:
        kr = axial_rope(kr, tabs)
    k_rope = jnp.broadcast_to(kr[:, :, None, :], (b, l, MLA_HEADS, MLA_ROPE))
    return jnp.concatenate([kv[..., :MLA_NOPE], k_rope], axis=-1), kv[..., MLA_NOPE:]


def block_attention(q, k, v):
    b, s, h, dq = q.shape
    nb = s // Q_BLOCK
    scale = dq ** -0.5
    qb = q.reshape(b, nb, Q_BLOCK, h, dq).transpose(1, 0, 2, 3, 4)

    def one(qi):
        sc = jnp.einsum('bqhd,bkhd->bhqk', qi, k, preferred_element_type=jnp.float32) * scale
        p = jax.nn.softmax(sc, axis=-1).astype(v.dtype)
        return jnp.einsum('bhqk,bkhd->bqhd', p, v)

    o = lax.map(one, qb)
    return o.transpose(1, 0, 2, 3, 4).reshape(b, s, h * v.shape[-1])


def dn_prepare(qkv, a, bt, conv_w, a_log, dt_bias):
    b, l, _ = qkv.shape
    qkv = jax.nn.silu(dwconv(qkv, conv_w)).reshape(b, l, 3, DN_HEADS, DN_DH)
    q = l2_norm(qkv[:, :, 0])
    k = l2_norm(qkv[:, :, 1])
    v = qkv[:, :, 2].astype(jnp.float32)
    a = a.reshape(b, l, 2, DN_HEADS).astype(jnp.float32)
    bt = bt.reshape(b, l, 2, DN_HEADS).astype(jnp.float32)
    g = -jnp.exp(a_log.astype(jnp.float32)) * jax.nn.softplus(a + dt_bias.astype(jnp.float32))
    beta = jax.nn.sigmoid(bt)
    return q, k, v, g, beta


def gated_delta_chunked(q, k, v, g, beta, s0, with_output):
    b, l, h, dk = q.shape
    c = DN_CHUNK
    n = l // c

    def chunks(t):
        return t.reshape(b, n, c, h, t.shape[-1]).transpose(1, 0, 3, 2, 4)

    qc = chunks(q * dk ** -0.5)
    kc = chunks(k)
    vc = chunks(v)
    gc = jnp.cumsum(g.reshape(b, n, c, h).transpose(1, 0, 3, 2), axis=-1)
    bc = beta.reshape(b, n, c, h).transpose(1, 0, 3, 2)
    incl = jnp.tril(jnp.ones((c, c), dtype=bool))
    strict = jnp.tril(jnp.ones((c, c), dtype=bool), -1)
    gamma = jnp.exp(jnp.where(incl, gc[..., :, None] - gc[..., None, :], -jnp.inf))
    kb = kc * bc[..., None]
    a_mat = jnp.where(strict, jnp.einsum('nbhcd,nbhed->nbhce', kb, kc) * gamma, 0.0)
    eye = jnp.eye(c, dtype=q.dtype)
    t_mat = lax.linalg.triangular_solve(eye + a_mat, jnp.broadcast_to(eye, a_mat.shape), left_side=True, lower=True)
    u = jnp.einsum('nbhce,nbhed->nbhcd', t_mat, vc * bc[..., None])
    w = jnp.einsum('nbhce,nbhed->nbhcd', t_mat, kb * jnp.exp(gc)[..., None])
    g_last = gc[..., -1]
    k_tail = kc * jnp.exp(g_last[..., None] - gc)[..., None]
    xs = (w, u, k_tail, g_last)
    if with_output:
        q_dec = qc * jnp.exp(gc)[..., None]
        qk = jnp.einsum('nbhcd,nbhed->nbhce', qc, kc) * gamma
        xs = xs + (q_dec, qk)

    def step(s, xi):
        w_i, u_i, kt_i, gl_i = xi[:4]
        v_new = u_i - jnp.einsum('bhcd,bhde->bhce', w_i, s)
        s_next = s * jnp.exp(gl_i)[..., None, None] + jnp.einsum('bhcd,bhce->bhde', kt_i, v_new)
        if with_output:
            qd_i, qk_i = xi[4:]
            return s_next, jnp.einsum('bhcd,bhde->bhce', qd_i, s) + jnp.einsum('bhce,bhed->bhcd', qk_i, v_new)
        return s_next, None

    s_fin, o = lax.scan(step, s0, xs)
    if with_output:
        o = o.transpose(1, 0, 3, 2, 4).reshape(b, l, h, v.shape[-1])
    return o, s_fin


def _flip(t, d):
    return jnp.flip(t, axis=1) if d == 1 else t


def deltanet_bidir(pc, px, need_ctx):
    qc, kc, vc, gc, bc = pc
    qx, kx, vx, gx, bx = px
    bsz = qx.shape[0]
    outs_c, outs_x = [], []
    for d in range(2):
        s0 = jnp.zeros((bsz, DN_HEADS, DN_DH, DN_DH), jnp.float32)
        oc, s_ctx = gated_delta_chunked(_flip(qc, d), _flip(kc, d), _flip(vc, d), _flip(gc[:, :, d], d),
                                        _flip(bc[:, :, d], d), s0, need_ctx)
        ox, _ = gated_delta_chunked(_flip(qx, d), _flip(kx, d), _flip(vx, d), _flip(gx[:, :, d], d),
                                    _flip(bx[:, :, d], d), s_ctx, True)
        outs_x.append(_flip(ox, d))
        if need_ctx:
            outs_c.append(_flip(oc, d))
    o_c = outs_c[0] + outs_c[1] if need_ctx else None
    return o_c, outs_x[0] + outs_x[1]


def dn_output(o, z, norm_g):
    b, l, _ = z.shape
    y = rms_norm(o, norm_g) * jax.nn.silu(z.reshape(b, l, DN_HEADS, DN_DH).astype(jnp.float32))
    return y.reshape(b, l, DN_W).astype(z.dtype)


def fourier_mix(u):
    b, l, _ = u.shape
    ug = u.reshape(b, l, FN_GROUPS, FN_GW).astype(jnp.float32)
    y = jnp.fft.fft2(ug, axes=(1, 3), norm='ortho').real
    return y.reshape(b, l, FN_W).astype(u.dtype)


def merge_branches(o_a, o_b, o_f, g_lr, w_gate_up, b_gate, w_br_a, w_br_b, w_br_c, w_out):
    b, l, _ = o_a.shape
    gates = jax.nn.sigmoid(g_lr @ w_gate_up + b_gate).reshape(b, l, N_BRANCH, D_MODEL)
    y = gates[:, :, 0] * (o_a @ w_br_a) + gates[:, :, 1] * (o_b @ w_br_b) + gates[:, :, 2] * (o_f @ w_br_c)
    return y @ w_out


def conv_ffn(h, w_up, conv_w, w_down):
    gate, up = jnp.split(dwconv(h @ w_up, conv_w), 2, axis=-1)
    return (jax.nn.silu(gate) * up) @ w_down


def hybrid_layer(x, xc, mod, mod_c, tabs, g_mix, g_ffn, w_in, q_norm_g, kv_norm_g, w_uq, w_ukv,
                 dn_conv, dn_a_log, dn_dt_bias, dn_norm_g, w_gate_up, b_gate, w_br_a, w_br_b, w_br_c, w_out,
                 w_ffn_up, ffn_conv, w_ffn_down, need_ctx):
    sh1, sc1, gt1, sh2, sc2, gt2 = (m[:, None, :] for m in jnp.split(mod, 6, axis=-1))
    sh1c, sc1c, gt1c, sh2c, sc2c, gt2c = jnp.split(mod_c, 6, axis=-1)
    h = modulate(rms_norm(x, g_mix), sh1, sc1)
    hc = modulate(rms_norm(xc, g_mix), sh1c, sc1c)
    ckv, kr, qkv, a, bt, cq, z, u_fn, g_lr = jnp.split(h @ w_in, IN_SPLITS, axis=-1)
    if need_ctx:
        ckv_c, kr_c, qkv_c, a_c, bt_c, cq_c, z_c, u_fn_c, g_lr_c = jnp.split(hc @ w_in, IN_SPLITS, axis=-1)
    else:
        ckv_c, kr_c, qkv_c, a_c, bt_c = jnp.split(hc @ w_in[:, :STATE_COLS], STATE_SPLITS, axis=-1)

    k_c, v_c = mla_keys_values(ckv_c, kr_c, kv_norm_g, w_ukv, None)
    k_x, v_x = mla_keys_values(ckv, kr, kv_norm_g, w_ukv, tabs)
    q_x = mla_queries(cq, q_norm_g, w_uq, tabs)
    o_a = block_attention(q_x, jnp.concatenate([k_c, k_x], axis=1), jnp.concatenate([v_c, v_x], axis=1))

    pc = dn_prepare(qkv_c, a_c, bt_c, dn_conv, dn_a_log, dn_dt_bias)
    px = dn_prepare(qkv, a, bt, dn_conv, dn_a_log, dn_dt_bias)
    oc_dn, ox_dn = deltanet_bidir(pc, px, need_ctx)
    o_b = dn_output(ox_dn, z, dn_norm_g)

    o_f = fourier_mix(u_fn)

    x = x + gt1 * merge_branches(o_a, o_b, o_f, g_lr, w_gate_up, b_gate, w_br_a, w_br_b, w_br_c, w_out)
    if need_ctx:
        o_ac = block_attention(mla_queries(cq_c, q_norm_g, w_uq, None), k_c, v_c)
        o_bc = dn_output(oc_dn, z_c, dn_norm_g)
        o_fc = fourier_mix(u_fn_c)
        xc = xc + gt1c * merge_branches(o_ac, o_bc, o_fc, g_lr_c, w_gate_up, b_gate, w_br_a, w_br_b, w_br_c, w_out)

    x = x + gt2 * conv_ffn(modulate(rms_norm(x, g_ffn), sh2, sc2), w_ffn_up, ffn_conv, w_ffn_down)
    if need_ctx:
        xc = xc + gt2c * conv_ffn(modulate(rms_norm(xc, g_ffn), sh2c, sc2c), w_ffn_up, ffn_conv, w_ffn_down)
    return x, xc


def setup_inputs(seed: int = 0) -> dict:
    key = jax.random.key(seed)
    ks = iter(jax.random.split(key, 40))
    f32 = jnp.float32
    L = DEPTH
    D = D_MODEL

    def nrm(shape, fan_in):
        return jax.random.normal(next(ks), shape, f32) * (fan_in ** -0.5)

    def gain(shape):
        return 1.0 + 0.02 * jax.random.normal(next(ks), shape, f32)

    def bias(shape):
        return 0.02 * jax.random.normal(next(ks), shape, f32)

    x = jax.random.normal(next(ks), (BATCH, SEQ, D), f32)
    c = jax.random.normal(next(ks), (BATCH, D), f32)
    ctx = jax.random.normal(next(ks), (BATCH, CTX_LEN, D), f32)
    c_ctx = jax.random.normal(next(ks), (D,), f32)
    w_ada = nrm((L, D, 6 * D), D)
    b_ada = bias((L, 6 * D))
    g_mix = gain((L, D))
    g_ffn = gain((L, D))
    w_in = nrm((L, D, IN_COLS), D)
    q_norm_g = gain((L, Q_LORA))
    kv_norm_g = gain((L, KV_LORA))
    w_uq = nrm((L, Q_LORA, MLA_HEADS * MLA_QK), Q_LORA)
    w_ukv = nrm((L, KV_LORA, MLA_HEADS * (MLA_NOPE + MLA_V)), KV_LORA)
    dn_conv = nrm((L, DN_CONV, 3 * DN_W), DN_CONV)
    dn_a_log = jnp.log(jax.random.uniform(next(ks), (L, 2, DN_HEADS), f32, minval=1.0, maxval=16.0))
    dt = jnp.exp(jax.random.uniform(next(ks), (L, 2, DN_HEADS), f32, minval=float(np.log(1e-3)), maxval=float(np.log(1e-1))))
    dn_dt_bias = dt + jnp.log(-jnp.expm1(-dt))
    dn_norm_g = gain((L, DN_DH))
    w_gate_up = nrm((L, GATE_RANK, N_BRANCH * D), GATE_RANK)
    b_gate = bias((L, N_BRANCH * D))
    w_br_a = nrm((L, MLA_OUT, D), MLA_OUT)
    w_br_b = nrm((L, DN_W, D), DN_W)
    w_br_c = nrm((L, FN_W, D), FN_W)
    w_out = nrm((L, D, D), D)
    w_ffn_up = nrm((L, D, 2 * D_FF), D)
    ffn_conv = nrm((L, FFN_CONV, 2 * D_FF), FFN_CONV)
    w_ffn_down = nrm((L, D_FF, D), D_FF)
    g_final = gain((D,))
    return {"x": x, "c": c, "ctx": ctx, "c_ctx": c_ctx, "w_ada": w_ada, "b_ada": b_ada, "g_mix": g_mix,
            "g_ffn": g_ffn, "w_in": w_in, "q_norm_g": q_norm_g, "kv_norm_g": kv_norm_g, "w_uq": w_uq,
            "w_ukv": w_ukv, "dn_conv": dn_conv, "dn_a_log": dn_a_log, "dn_dt_bias": dn_dt_bias,
            "dn_norm_g": dn_norm_g, "w_gate_up": w_gate_up, "b_gate": b_gate, "w_br_a": w_br_a,
            "w_br_b": w_br_b, "w_br_c": w_br_c, "w_out": w_out, "w_ffn_up": w_ffn_up, "ffn_conv": ffn_conv,
            "w_ffn_down": w_ffn_down, "g_final": g_final}


def reference(x, c, ctx, c_ctx, w_ada, b_ada, g_mix, g_ffn, w_in, q_norm_g, kv_norm_g, w_uq, w_ukv, dn_conv,
              dn_a_log, dn_dt_bias, dn_norm_g, w_gate_up, b_gate, w_br_a, w_br_b, w_br_c, w_out, w_ffn_up,
              ffn_conv, w_ffn_down, g_final):
    c_s = jax.nn.silu(c)
    cc_s = jax.nn.silu(c_ctx)
    tabs = axial_rope_tables(x.shape[1], x.dtype)
    xc = ctx
    for i in range(DEPTH):
        mod = c_s @ w_ada[i] + b_ada[i]
        mod_c = cc_s @ w_ada[i] + b_ada[i]
        x, xc = hybrid_layer(x, xc, mod, mod_c, tabs, g_mix[i], g_ffn[i], w_in[i], q_norm_g[i], kv_norm_g[i],
                             w_uq[i], w_ukv[i], dn_conv[i], dn_a_log[i], dn_dt_bias[i], dn_norm_g[i],
                             w_gate_up[i], b_gate[i], w_br_a[i], w_br_b[i], w_br_c[i], w_out[i],
                             w_ffn_up[i], ffn_conv[i], w_ffn_down[i], i < DEPTH - 1)
    return rms_norm(x, g_final)
```

```python
import math
import numpy as np
import concourse.bass as bass
import concourse.mybir as mybir
from concourse.bass_utils import run_bass_kernel_spmd

F32 = mybir.dt.float32
BF16 = mybir.dt.bfloat16
AF = mybir.ActivationFunctionType
ALU = mybir.AluOpType
AX = mybir.AxisListType
P = 128
EPS = 1e-6


def make_cfg(D=4096, B=2, SL=4096, CL=256, DEPTH=4):
    c = dict(D=D, B=B, SL=SL, CL=CL, DEPTH=DEPTH)
    c['H'] = D // 256; c['NOPE'] = 128; c['ROPE'] = 64; c['QK'] = 192; c['VD'] = 128
    c['QL'] = D // 4; c['KVL'] = D // 8
    c['DH'] = D // 512; c['DNW'] = c['DH'] * 128
    c['FG'] = 4; c['GW'] = D // 16; c['FNW'] = 4 * c['GW']
    c['GR'] = D // 16; c['DFF'] = D * 5 // 4
    c['TB'] = CL + SL; c['T'] = B * (CL + SL)
    kv, r, dw, dh = c['KVL'], 64, c['DNW'], c['DH']
    o = {}
    o['ckv'] = 0; o['kr'] = kv; o['qkv'] = kv + r; o['a'] = kv + r + 3 * dw; o['bt'] = o['a'] + 2 * dh
    o['cq'] = o['bt'] + 2 * dh; o['z'] = o['cq'] + c['QL']; o['u'] = o['z'] + dw; o['glr'] = o['u'] + c['FNW']
    c['off'] = o; c['INC'] = o['glr'] + c['GR']
    return c


class Sched:
    NDS = 48

    def __init__(self, nc, sems):
        self.nc = nc
        self.eng = {'pe': nc.tensor, 'dve': nc.vector, 'act': nc.scalar, 'sp': nc.sync, 'pool': nc.gpsimd}
        self.dma = ('sp', 'pool')
        self.comp = ('pe', 'dve', 'act')
        self.sems = sems
        self.count = {e: 0 for e in self.comp}
        self.dcount = [0] * self.NDS
        self.dkey2sem = {}
        self.dnext = 0
        self.waited = {e: {} for e in self.eng}
        self.lastw = {}
        self.readers = {}
        self.ninst = 0

    def _wait(self, e, ident):
        if ident[0] == 'D':
            k = ('D', ident[1]); v = ident[2]
            sem, val = self.sems['D'][ident[1]], 16 * ident[2]
        else:
            k = ident[0]; v = ident[1]
            sem, val = self.sems[ident[0]][0], ident[1] + 1
        if self.waited[e].get(k, -1) >= v:
            return
        self.waited[e][k] = v
        self.eng[e].wait_ge(sem, val)

    def emit(self, e, fn, reads=(), writes=()):
        reads = list(reads); writes = list(writes)
        dkey = (writes[0] if writes else (reads[0] if reads else '_misc')) if e in self.dma else None
        writes = writes + [r for r in reads if isinstance(r, str) and r.startswith('ps') and r not in writes]
        deps = set()
        for r in reads:
            lw = self.lastw.get(r)
            if lw is not None:
                deps.add(lw)
        for w in writes:
            lw = self.lastw.get(w)
            if lw is not None:
                deps.add(lw)
            deps.update(self.readers.get(w, ()))
        best = {}
        for d in deps:
            k = d[:2] if d[0] == 'D' else d[:1]
            if k not in best or best[k][-1] < d[-1]:
                best[k] = d
        for d in best.values():
            if d[0] == 'pe' and e == 'pe':
                continue
            self._wait(e, d)
        if e in self.dma:
            si = self.dkey2sem.get(dkey)
            if si is None:
                si = self.dnext % self.NDS; self.dnext += 1
                self.dkey2sem[dkey] = si
            self.dcount[si] += 1
            ident = ('D', si, self.dcount[si])
            inst = fn(self.eng[e])
            inst.then_inc(self.sems['D'][si], 16)
        else:
            ident = (e, self.count[e])
            self.count[e] += 1
            inst = fn(self.eng[e])
            inst.then_inc(self.sems[e][0], 1)
        self.ninst += 1
        for w in writes:
            self.lastw[w] = ident
            self.readers[w] = set()
        for r in reads:
            d = self.readers.setdefault(r, set())
            if e not in self.dma:
                for x in [x for x in d if x[0] == e]:
                    d.discard(x)
            d.add(ident)

    def barrier(self):
        for e in self.eng:
            for f in self.comp:
                if self.count[f] > 0:
                    self._wait(e, (f, self.count[f] - 1))
            for si in range(self.NDS):
                if self.dcount[si] > 0:
                    self._wait(e, ('D', si, self.dcount[si]))
        self.lastw.clear()
        self.readers.clear()
        self.dkey2sem.clear()
        self.dnext = 0


_KC_CACHE = {}


def host_consts(cfg):
    key = (cfg['SL'], cfg['CL'], cfg['GW'])
    if key not in _KC_CACHE:
        _KC_CACHE[key] = _host_consts(cfg)
    return _KC_CACHE[key]


def _host_consts(cfg):
    SL, CL, TB, GW = cfg['SL'], cfg['CL'], cfg['TB'], cfg['GW']
    k = {}
    t = np.arange(SL)
    row = (t // 64).astype(np.float32); col = (t % 64).astype(np.float32)
    half = 32
    inv = (10000.0 ** (-np.arange(0, half, 2, dtype=np.float32) / half)).astype(np.float32)
    ar = (row[:, None] * inv[None, :]).astype(np.float32); ac = (col[:, None] * inv[None, :]).astype(np.float32)
    cos = np.ones((64, TB), np.float32); sin = np.zeros((64, TB), np.float32)
    for d in range(64):
        g, e = d // 32, d % 32
        i = e % 16
        ang = ar[:, i] if g == 0 else ac[:, i]
        cos[d, CL:] = np.cos(ang); sin[d, CL:] = np.sin(ang)
    k['ropecos'] = cos; k['ropesin'] = sin
    R = np.zeros((64, 64), np.float32)
    for m in range(64):
        e = m % 32
        if e < 16:
            R[m, m + 16] = -1.0
        else:
            R[m, m - 16] = 1.0
    k['rotT'] = np.ascontiguousarray(R.T)
    k['ident'] = np.eye(128, dtype=np.float32)
    k['ones'] = np.ones((128, 128), np.float32)
    i = np.arange(128)[:, None]; j = np.arange(128)[None, :]
    same = (i // 64) == (j // 64)
    BIG = 30000.0
    L = same & (i >= j); U = same & (i <= j)
    k['penL'] = np.where(L, 0.0, BIG).astype(np.float32)
    k['penU'] = np.where(U, 0.0, BIG).astype(np.float32)
    k['strictL'] = (same & (i > j)).astype(np.float32)
    k['strictU'] = (same & (i < j)).astype(np.float32)
    k['tri0'] = (same & (j >= i)).astype(np.float32)
    k['tri1'] = (same & (j <= i)).astype(np.float32)
    s0 = np.zeros((128, 128), np.float32); s1 = np.zeros((128, 128), np.float32)
    for mm in range(128):
        s0[(mm // 64) * 64 + 63, mm] = 1.0
        s1[(mm // 64) * 64, mm] = 1.0
    k['sel0'] = s0; k['sel1'] = s1
    pk = np.zeros((4, 128, 128), np.float32)
    for n, r in enumerate((63, 127, 0, 64)):
        pk[n, r, :] = 1.0
    k['pick'] = pk
    rmk = np.zeros((128, 2), np.float32); rmk[:64, 0] = 1.0; rmk[64:, 1] = 1.0
    k['rowmask'] = rmk
    for n in sorted({SL, CL, GW}):
        a = np.arange(n, dtype=np.int64)
        m = (a[:, None] * a[None, :]) % n
        ang = 2.0 * np.pi * m.astype(np.float64) / n
        k['dftc%d' % n] = (np.cos(ang) / np.sqrt(n)).astype(np.float32)
        k['dfts%d' % n] = (np.sin(ang) / np.sqrt(n)).astype(np.float32)
    return k


def build(cfg, dbg=()):
    D, B, SL, CL, TB, T, NL = cfg['D'], cfg['B'], cfg['SL'], cfg['CL'], cfg['TB'], cfg['T'], cfg['DEPTH']
    H, QL, KVL, DH, DNW, GW, FNW, GR, DFF = (cfg[k] for k in ('H', 'QL', 'KVL', 'DH', 'DNW', 'GW', 'FNW', 'GR', 'DFF'))
    off, INC = cfg['off'], cfg['INC']
    DC = D // P
    nc = bass.Bass("TRN2", target_bir_lowering=False)
    ins = {}

    def inp(name, shape):
        ins[name] = nc.dram_tensor(name, list(shape), F32, kind="ExternalInput").ap()
        return ins[name]

    x_in = inp('x', (B, SL, D)); c_in = inp('c', (B, D)); ctx_in = inp('ctx', (B, CL, D)); cctx_in = inp('c_ctx', (1, D))
    w_ada = inp('w_ada', (NL, D, 6 * D)); b_ada = inp('b_ada', (NL, 6 * D)); g_mix = inp('g_mix', (NL, D)); g_ffn = inp('g_ffn', (NL, D))
    w_in = inp('w_in', (NL, D, INC)); q_norm_g = inp('q_norm_g', (NL, QL)); kv_norm_g = inp('kv_norm_g', (NL, KVL))
    w_uq = inp('w_uq', (NL, QL, H * 192)); w_ukv = inp('w_ukv', (NL, KVL, H * 256)); dn_conv = inp('dn_conv', (NL, 5, 3 * DNW))
    dn_a_log = inp('dn_a_log', (NL, 2 * DH)); dn_dt_bias = inp('dn_dt_bias', (NL, 2 * DH)); dn_norm_g = inp('dn_norm_g', (NL, 128))
    w_gate_up = inp('w_gate_up', (NL, GR, 3 * D)); b_gate = inp('b_gate', (NL, 3 * D))
    w_br_a = inp('w_br_a', (NL, H * 128, D)); w_br_b = inp('w_br_b', (NL, DNW, D)); w_br_c = inp('w_br_c', (NL, FNW, D))
    w_out = inp('w_out', (NL, D, D)); w_ffn_up = inp('w_ffn_up', (NL, D, 2 * DFF)); ffn_conv = inp('ffn_conv', (NL, 3, 2 * DFF))
    w_ffn_down = inp('w_ffn_down', (NL, DFF, D)); g_final = inp('g_final', (1, D))
    kc = host_consts(cfg)
    for n, v in kc.items():
        inp('k_' + n, v.shape)
    out_ap = nc.dram_tensor('out', [B, SL, D], F32, kind="ExternalOutput").ap()

    def scratch(name, shape, dt):
        kind = "ExternalOutput" if name in dbg else "Internal"
        return nc.dram_tensor(name, list(shape), dt, kind=kind).ap()

    xT = scratch('xT', (D, T), F32)
    hT = scratch('hT', (D, T), BF16)
    ckvT = scratch('ckvT', (KVL, T), F32); ckvnT = scratch('ckvnT', (KVL, T), BF16)
    krT = scratch('krT', (64, T), BF16); krrT = scratch('krrT', (64, T), BF16)
    qkvT = scratch('qkvT', (3 * DNW, T), F32)
    abtm = scratch('abtm', (T, 4 * DH), F32)
    cqT = scratch('cqT', (QL, T), F32); cqnT = scratch('cqnT', (QL, T), BF16)
    ztm = scratch('ztm', (T, DNW), F32)
    uT = scratch('uT', (FNW, T), BF16)
    glrT = scratch('glrT', (GR, T), BF16)
    qT = scratch('qT', (H * 192, T), BF16); qrT = scratch('qrT', (H * 64, T), BF16); qnT = scratch('qnT', (H * 128, T), BF16)
    kvT = scratch('kvT', (H * 256, T), BF16); kvtm = scratch('kvtm', (T, H * 256), BF16)
    oaT = scratch('oaT', (H * 128, T), BF16); obT = scratch('obT', (DNW, T), BF16); ofT = scratch('ofT', (FNW, T), BF16)
    yT = scratch('yT', (D, T), BF16)
    upT = scratch('upT', (2 * DFF, T), BF16); mT = scratch('mT', (DFF, T), BF16)

    def tblocks(bs):
        r = []
        for b in range(B):
            for (s0, n, seg) in ((b * TB, CL, B), (b * TB + CL, SL, b)):
                for o in range(0, n, bs):
                    r.append((s0 + o, min(bs, n - o), seg))
        return r
    segs = []
    for b in range(B):
        segs.append((b * TB, CL, B)); segs.append((b * TB + CL, SL, b))

    from contextlib import ExitStack
    with ExitStack() as gs:
        sems = {e: [gs.enter_context(nc.semaphore('s_%s' % e))] for e in ('pe', 'dve', 'act')}
        sems['D'] = [gs.enter_context(nc.semaphore('s_d%d' % i)) for i in range(Sched.NDS)]
        S = Sched(nc, sems)
        ps = [gs.enter_context(nc.psum_tensor('ps%d' % i, [P, 512], F32)) for i in range(8)]
        PSK = ['ps%d' % i for i in range(8)]

        uniq = [0]

        def sb(st, name, shape, dt):
            uniq[0] += 1
            return st.enter_context(nc.sbuf_tensor('%s_u%d' % (name, uniq[0]), list(shape), dt))

        def dma(q, out, in_, reads, writes, **kw):
            S.emit(q, lambda e: e.dma_start(out=out, in_=in_, **kw), reads, writes)

        def mm(out, lhsT, rhs, start, stop, reads, writes):
            S.emit('pe', lambda e: e.matmul(out, lhsT, rhs, start=start, stop=stop, skip_group_check=True), reads, writes)

        def tr(out, in_, ident, reads, writes):
            S.emit('pe', lambda e: e.transpose(out, in_, ident), reads, writes)

        def act(out, in_, func, reads, writes, bias=None, scale=None, eng='act'):
            kw = {}
            if bias is not None:
                kw['bias'] = bias
            if scale is not None:
                kw['scale'] = scale
            S.emit(eng, lambda e: e.activation(out, in_, func, **kw), reads, writes)

        def ts(out, in0, s1, s2, op0, op1, reads, writes):
            if op1 is None:
                S.emit('dve', lambda e: e.tensor_scalar(out, in0, s1, None, op0), reads, writes)
            else:
                S.emit('dve', lambda e: e.tensor_scalar(out, in0, s1, s2, op0, op1), reads, writes)

        def stt(out, in0, sc, in1, op0, op1, reads, writes):
            S.emit('dve', lambda e: e.scalar_tensor_tensor(out, in0, sc, in1, op0, op1), reads, writes)

        def tt(out, in0, in1, op, reads, writes):
            S.emit('dve', lambda e: e.tensor_tensor(out, in0, in1, op), reads, writes)

        def cp(out, in_, reads, writes, eng='dve'):
            if eng == 'dve':
                S.emit('dve', lambda e: e.tensor_copy(out, in_), reads, writes)
            else:
                S.emit('act', lambda e: e.copy(out, in_), reads, writes)

        ident = sb(gs, 'ident', (P, P), F32); ones = sb(gs, 'ones', (P, P), F32)
        identb = sb(gs, 'identb', (P, P), BF16); onesb = sb(gs, 'onesb', (P, P), BF16)
        dma('sp', ident[:], ins['k_ident'][:, :], [], ['ident']); dma('sp', ones[:], ins['k_ones'][:, :], [], ['ones'])
        dma('pool', identb[:], ins['k_ident'][:, :], [], ['identb']); dma('pool', onesb[:], ins['k_ones'][:, :], [], ['onesb'])
        modT = sb(gs, 'modT', (P, 6 * DC, 4), F32)
        gsT = sb(gs, 'gsT', (P, 2, DC, 4), F32)
        csT = sb(gs, 'csT', (P, DC, 4), BF16)
        evk = [0]

        def load_vec_T(st, dst, dkey, src_row, n, tmpname):
            nj = n // P
            for j0 in range(0, nj, P):
                r = min(P, nj - j0)
                tmp = sb(st, '%s_%d' % (tmpname, j0), (P, P), F32)
                k = '%s_%d' % (tmpname, j0)
                dma('sp', tmp[:r, :], src_row[j0 * P:(j0 + r) * P].rearrange("(j p) -> j p", p=P), [], [k])
                tr(ps[7][:, :r], tmp[:r, :], ident[:r, :r], [k, 'ident'], ['ps7'])
                cp(dst[:, j0:j0 + r], ps[7][:, :r], ['ps7'], [dkey])

        def gemm(st, name, actT, K, W, N, blocks, handler, token_major=False, CB=512):
            KC = K // P
            wsb = [sb(st, '%s_w%d' % (name, i), (P, KC, CB), BF16) for i in range(2)]
            asb = [sb(st, '%s_a%d' % (name, i), (P, KC, 512), BF16) for i in range(2)]
            Wv = W.rearrange("(p j) n -> p j n", j=KC); Av = actT.rearrange("(p j) t -> p j t", j=KC)
            ai = 0
            for ci, c0 in enumerate(range(0, N, CB)):
                cn = min(CB, N - c0)
                wk = '%s_w%d' % (name, ci % 2)
                dma('pool', wsb[ci % 2][:, :, :cn], Wv[:, :, c0:c0 + cn], [], [wk])
                for (t0, tn, seg) in blocks:
                    ak = '%s_a%d' % (name, ai % 2); a = asb[ai % 2]; ai += 1
                    dma('sp', a[:, :, :tn], Av[:, :, t0:t0 + tn], [], [ak])
                    if not token_major:
                        for s0 in range(0, cn, P):
                            m = min(P, cn - s0)
                            bi = evk[0] % 4; evk[0] += 1
                            for j in range(KC):
                                mm(ps[bi][:m, :tn], wsb[ci % 2][:, j, s0:s0 + m], a[:, j, :tn], j == 0, j == KC - 1, [wk, ak], [PSK[bi]])
                            handler(c0 + s0, m, t0, tn, seg, ps[bi], PSK[bi])
                    else:
                        for s0 in range(0, tn, P):
                            bi = evk[0] % 4; evk[0] += 1
                            for j in range(KC):
                                mm(ps[bi][:, :cn], a[:, j, s0:s0 + P], wsb[ci % 2][:, j, :cn], j == 0, j == KC - 1, [wk, ak], [PSK[bi]])
                            handler(c0, cn, t0 + s0, P, seg, ps[bi], PSK[bi])

        def store_handler(st, name, dstT, dt, row0=0, func=None):
            og = [sb(st, '%s_o%d' % (name, i), (P, 512), dt) for i in range(3)]
            cnt = [0]

            def h(c0, m, t0, tn, seg, pst, pk):
                i = cnt[0] % 3; cnt[0] += 1
                ok = '%s_o%d' % (name, i)
                if func is not None:
                    act(og[i][:m, :tn], pst[:m, :tn], func, [pk], [ok])
                elif cnt[0] % 2:
                    cp(og[i][:m, :tn], pst[:m, :tn], [pk], [ok])
                else:
                    cp(og[i][:m, :tn], pst[:m, :tn], [pk], [ok], eng='act')
                dma('sp', dstT[row0 + c0:row0 + c0 + m, t0:t0 + tn], og[i][:m, :tn], [ok], [])
            return h

        def store_handler_tm(st, name, dst_tm, dt, col0=0):
            og = [sb(st, '%s_o%d' % (name, i), (P, 512), dt) for i in range(3)]
            cnt = [0]

            def h(c0, cn, t0, tn, seg, pst, pk):
                i = cnt[0] % 3; cnt[0] += 1
                ok = '%s_o%d' % (name, i)
                cp(og[i][:, :cn], pst[:, :cn], [pk], [ok], eng=('dve' if cnt[0] % 2 else 'act'))
                dma('sp', dst_tm[t0:t0 + P, col0 + c0:col0 + c0 + cn], og[i][:, :cn], [ok], [])
            return h

        def norm_stage(name, srcT, dstT, Fdim, gs_ap, sh_ap):
            FC = Fdim // P
            NB = 256
            with ExitStack() as st:
                xs = [sb(st, name + '_x%d' % i, (P, FC, NB), F32) for i in range(2)]
                sq = sb(st, name + '_sq', (P, FC, NB), F32)
                ho = [sb(st, name + '_h%d' % i, (P, FC, NB), BF16) for i in range(2)]
                rs = sb(st, name + '_rs', (P, NB), F32)
                sv = srcT.rearrange("(c p) t -> p c t", p=P); dv = dstT.rearrange("(c p) t -> p c t", p=P)
                for bi_, (t0, tn, seg) in enumerate(tblocks(NB)):
                    i = bi_ % 2
                    xk = name + '_x%d' % i; hk = name + '_h%d' % i
                    dma('sp', xs[i][:, :, :tn], sv[:, :, t0:t0 + tn], [], [xk])
                    act(sq[:, :, :tn], xs[i][:, :, :tn], AF.Square, [xk], [name + '_sq'])
                    for c in range(FC):
                        mm(ps[4][:, :tn], ones[:, :], sq[:, c, :tn], c == 0, c == FC - 1, ['ones', name + '_sq'], ['ps4'])
                    act(rs[:, :tn], ps[4][:, :tn], AF.Sqrt, ['ps4'], [name + '_rs'], bias=EPS, scale=1.0 / Fdim)
                    S.emit('dve', lambda e: e.reciprocal(rs[:, :tn], rs[:, :tn]), [name + '_rs'], [name + '_rs'])
                    for c in range(FC):
                        tt(xs[i][:, c, :tn], xs[i][:, c, :tn], rs[:, :tn], ALU.mult, [xk, name + '_rs'], [xk])
                        if sh_ap is not None:
                            ts(ho[i][:, c, :tn], xs[i][:, c, :tn], gs_ap[:, c, seg:seg + 1], sh_ap[:, c, seg:seg + 1], ALU.mult, ALU.add,
                               [xk, 'gsT', 'modT'], [hk])
                        else:
                            ts(ho[i][:, c, :tn], xs[i][:, c, :tn], gs_ap[:, c:c + 1], None, ALU.mult, None, [xk, name + '_g'], [hk])
                    dma('sp', dv[:, :, t0:t0 + tn], ho[i][:, :, :tn], [hk], [])
                S.barrier()

        with ExitStack() as st:
            xin = [sb(st, 'xin%d' % i, (P, D), F32) for i in range(2)]
            xo = [sb(st, 'xo%d' % i, (P, DC, P), F32) for i in range(2)]
            xv = xT.rearrange("(c p) t -> p c t", p=P)
            n = 0
            for b in range(B):
                for (src, s0, ln) in ((ctx_in, b * TB, CL), (x_in, b * TB + CL, SL)):
                    for r0 in range(0, ln, P):
                        i = n % 2; n += 1
                        dma('sp', xin[i][:, :], src[b, r0:r0 + P, :], [], ['xin%d' % i])
                        for c4 in range(0, DC, 4):
                            bi = (c4 // 4) % 4
                            for c in range(c4, min(DC, c4 + 4)):
                                tr(ps[bi][:, (c - c4) * P:(c - c4 + 1) * P], xin[i][:, c * P:(c + 1) * P], ident[:, :], ['xin%d' % i, 'ident'], [PSK[bi]])
                            nn = min(DC, c4 + 4) - c4
                            cp(xo[i][:, c4:c4 + nn, :], ps[bi][:, :nn * P].rearrange("p (c t) -> p c t", t=P), [PSK[bi]], ['xo%d' % i],
                               eng=('dve' if (c4 // 4) % 2 else 'act'))
                        dma('sp', xv[:, :, s0 + r0:s0 + r0 + P], xo[i][:, :, :], ['xo%d' % i], [])
            craw = sb(st, 'craw', (P, 4, DC), F32)
            S.emit('dve', lambda e: e.memset(craw[:], 0.0), [], ['craw'])
            for b in range(B):
                dma('sp', craw[:, b, :], c_in[b, :].rearrange("(p j) -> p j", j=DC), [], ['craw'])
            dma('sp', craw[:, B, :], cctx_in[0, :].rearrange("(p j) -> p j", j=DC), [], ['craw'])
            act(craw[:], craw[:], AF.Silu, ['craw'], ['craw'])
            cp(csT[:], craw[:].rearrange("p r j -> p j r"), ['craw'], ['csT'])
            S.barrier()

        for li in range(NL):
            need_ctx = li < NL - 1
            with ExitStack() as st:
                badaT = sb(st, 'badaT', (P, 6 * DC), F32)
                load_vec_T(st, badaT, 'badaT', b_ada[li, :], 6 * D, 'tb')
                gT = sb(st, 'gT', (P, 2, DC), F32)
                load_vec_T(st, gT[:, 0, :], 'gT', g_mix[li, :], D, 'tg0')
                load_vec_T(st, gT[:, 1, :], 'gT', g_ffn[li, :], D, 'tg1')
                KC = DC
                wsb = [sb(st, 'ada_w%d' % i, (P, KC, 512), BF16) for i in range(2)]
                Wv = w_ada[li].rearrange("(p j) n -> p j n", j=KC)
                for ci, c0 in enumerate(range(0, 6 * D, 512)):
                    wk = 'ada_w%d' % (ci % 2)
                    dma('pool', wsb[ci % 2][:, :, :], Wv[:, :, c0:c0 + 512], [], [wk])
                    for s0 in range(0, 512, P):
                        bi = evk[0] % 4; evk[0] += 1
                        for j in range(KC):
                            mm(ps[bi][:, :4], wsb[ci % 2][:, j, s0:s0 + P], csT[:, j, :], j == 0, j == KC - 1, [wk, 'csT'], [PSK[bi]])
                        jj = (c0 + s0) // P
                        ts(modT[:, jj, :], ps[bi][:, :4], badaT[:, jj:jj + 1], None, ALU.add, None, [PSK[bi], 'badaT'], ['modT'])
                for w_, sc0 in ((0, DC), (1, 4 * DC)):
                    for c in range(DC):
                        ts(gsT[:, w_, c, :], modT[:, sc0 + c, :], 1.0, gT[:, w_, c:c + 1], ALU.add, ALU.mult, ['modT', 'gT'], ['gsT'])
                S.barrier()
            sh1 = modT[:, 0:DC, :]; gt1 = modT[:, 2 * DC:3 * DC, :]; sh2 = modT[:, 3 * DC:4 * DC, :]; gt2 = modT[:, 5 * DC:6 * DC, :]

            norm_stage('n1', xT, hT, D, gsT[:, 0, :, :], sh1)

            blk = tblocks(512)
            with ExitStack() as st:
                Wl = w_in[li]
                gemm(st, 'g1a', hT, D, Wl[:, off['ckv']:off['ckv'] + KVL], KVL, blk, store_handler(st, 'g1a', ckvT, F32))
                S.barrier()
            with ExitStack() as st:
                gemm(st, 'g1b', hT, D, Wl[:, off['kr']:off['kr'] + 64], 64, blk, store_handler(st, 'g1b', krT, BF16))
                S.barrier()
            with ExitStack() as st:
                gemm(st, 'g1c', hT, D, Wl[:, off['qkv']:off['qkv'] + 3 * DNW], 3 * DNW, blk, store_handler(st, 'g1c', qkvT, F32))
                S.barrier()
            with ExitStack() as st:
                gemm(st, 'g1d', hT, D, Wl[:, off['a']:off['a'] + 4 * DH], 4 * DH, blk, store_handler_tm(st, 'g1d', abtm, F32), token_major=True)
                S.barrier()
            with ExitStack() as st:
                gemm(st, 'g1e', hT, D, Wl[:, off['cq']:off['cq'] + QL], QL, blk, store_handler(st, 'g1e', cqT, F32))
                S.barrier()
            with ExitStack() as st:
                gemm(st, 'g1f', hT, D, Wl[:, off['z']:off['z'] + DNW], DNW, blk, store_handler_tm(st, 'g1f', ztm, F32), token_major=True)
                S.barrier()
            with ExitStack() as st:
                gemm(st, 'g1g', hT, D, Wl[:, off['u']:off['u'] + FNW], FNW, blk, store_handler(st, 'g1g', uT, BF16))
                S.barrier()
            with ExitStack() as st:
                gemm(st, 'g1h', hT, D, Wl[:, off['glr']:off['glr'] + GR], GR, blk, store_handler(st, 'g1h', glrT, BF16))
                S.barrier()

            with ExitStack() as st:
                qg = sb(st, 'qg', (P, QL // P), F32); kg = sb(st, 'kg', (P, KVL // P), F32)
                load_vec_T(st, qg, 'n2_g', q_norm_g[li, :], QL, 'tq')
                load_vec_T(st, kg, 'n3_g', kv_norm_g[li, :], KVL, 'tk')
                norm_stage('n2', cqT, cqnT, QL, qg, None)
                norm_stage('n3', ckvT, ckvnT, KVL, kg, None)
            with ExitStack() as st:
                gemm(st, 'g2', cqnT, QL, w_uq[li], H * 192, blk, store_handler(st, 'g2', qT, BF16))
                S.barrier()
            with ExitStack() as st:
                gemm(st, 'g3', ckvnT, KVL, w_ukv[li], H * 256, blk, store_handler(st, 'g3', kvT, BF16))
                S.barrier()
            with ExitStack() as st:
                gemm(st, 'g3t', ckvnT, KVL, w_ukv[li], H * 256, blk, store_handler_tm(st, 'g3t', kvtm, BF16), token_major=True)
                S.barrier()
            if cfg.get('stop_after') == 'proj':
                break
            with ExitStack() as st:
                rc = sb(st, 'rc', (64, TB), F32); rsn = sb(st, 'rsn', (64, TB), F32); rot = sb(st, 'rot', (64, 64), BF16)
                dma('sp', rc[:], ins['k_ropecos'][:, :], [], ['rc']); dma('sp', rsn[:], ins['k_ropesin'][:, :], [], ['rsn'])
                dma('pool', rot[:], ins['k_rotT'][:, :], [], ['rot'])
                xi = [sb(st, 'rxi%d' % i, (64, 512), BF16) for i in range(2)]
                t1 = sb(st, 'rt1', (64, 512), F32)
                xo = [sb(st, 'rxo%d' % i, (64, 512), BF16) for i in range(2)]
                n = 0
                for (src, row0, dst, drow0) in [(krT, 0, krrT, 0)] + [(qT, h * 192 + 128, qrT, h * 64) for h in range(H)]:
                    for (t0, tn, seg) in blk:
                        i = n % 2; n += 1
                        p0 = t0 % TB
                        dma('sp', xi[i][:, :tn], src[row0:row0 + 64, t0:t0 + tn], [], ['rxi%d' % i])
                        mm(ps[5][:64, :tn], rot[:, :], xi[i][:, :tn], True, True, ['rot', 'rxi%d' % i], ['ps5'])
                        tt(t1[:, :tn], xi[i][:, :tn], rc[:, p0:p0 + tn], ALU.mult, ['rxi%d' % i, 'rc'], ['rt1'])
                        tt(xo[i][:, :tn], ps[5][:64, :tn], rsn[:, p0:p0 + tn], ALU.mult, ['ps5', 'rsn'], ['rxo%d' % i])
                        tt(xo[i][:, :tn], xo[i][:, :tn], t1[:, :tn], ALU.add, ['rxo%d' % i, 'rt1'], ['rxo%d' % i])
                        dma('sp', dst[drow0:drow0 + 64, t0:t0 + tn], xo[i][:, :tn], ['rxo%d' % i], [])
                S.barrier()

            with ExitStack() as st:
                NKT = TB // P
                kn = sb(st, 'akn', (P, TB), BF16); kr_ = sb(st, 'akr', (64, TB), BF16)
                vt = sb(st, 'avt', (P, NKT, P), BF16)
                qn = sb(st, 'aqn', (P, TB), BF16); qr_ = sb(st, 'aqr', (64, TB), BF16)
                et = [sb(st, 'aet%d' % i, (P, 512), BF16) for i in range(3)]
                oo = [sb(st, 'aoo%d' % i, (P, 512), BF16) for i in range(2)]
                rden = sb(st, 'arden', (P, 512), F32)
                sc_ = 192.0 ** -0.5
                no = 0; ne = 0
                for b in range(B):
                    bs = b * TB
                    dma('sp', kr_[:], krrT[:, bs:bs + TB], [], ['akr'])
                    for h in range(H):
                        dma('sp', kn[:], kvT[h * 256:h * 256 + P, bs:bs + TB], [], ['akn'])
                        dma('sp', vt[:], kvtm[bs:bs + TB, h * 256 + P:h * 256 + 2 * P].rearrange("(c p) d -> p c d", p=P), [], ['avt'])
                        dma('sp', qn[:], qT[h * 192:h * 192 + P, bs:bs + TB], [], ['aqn'])
                        dma('sp', qr_[:], qrT[h * 64:h * 64 + 64, bs:bs + TB], [], ['aqr'])
                        qb = [(o, min(512, CL - o), CL) for o in range(0, CL, 512)] + [(o, 512, TB) for o in range(CL, TB, 512)]
                        for (q0, qn_, nk) in qb:
                            nkc = nk // P
                            for kc_ in range(nkc):
                                bi = 4 + kc_ % 2
                                mm(ps[bi][:, :qn_], kn[:, kc_ * P:(kc_ + 1) * P], qn[:, q0:q0 + qn_], True, False, ['akn', 'aqn'], [PSK[bi]])
                                mm(ps[bi][:, :qn_], kr_[:, kc_ * P:(kc_ + 1) * P], qr_[:, q0:q0 + qn_], False, True, ['akr', 'aqr'], [PSK[bi]])
                                ei = ne % 3; ne += 1
                                act(et[ei][:, :qn_], ps[bi][:, :qn_], AF.Exp, [PSK[bi]], ['aet%d' % ei], scale=sc_)
                                mm(ps[6][:, :qn_], vt[:, kc_, :], et[ei][:, :qn_], kc_ == 0, kc_ == nkc - 1, ['avt', 'aet%d' % ei], ['ps6'])
                                mm(ps[7][:, :qn_], onesb[:, :], et[ei][:, :qn_], kc_ == 0, kc_ == nkc - 1, ['onesb', 'aet%d' % ei], ['ps7'])
                            S.emit('dve', lambda e: e.reciprocal(rden[:, :qn_], ps[7][:, :qn_]), ['ps7'], ['arden'])
                            oi = no % 2; no += 1
                            tt(oo[oi][:, :qn_], ps[6][:, :qn_], rden[:, :qn_], ALU.mult, ['ps6', 'arden'], ['aoo%d' % oi])
                            dma('sp', oaT[h * P:(h + 1) * P, bs + q0:bs + q0 + qn_], oo[oi][:, :qn_], ['aoo%d' % oi], [])
                S.barrier()

            GC = GW // P
            for (L_, o_) in ((CL, 0), (SL, CL)):
                with ExitStack() as st:
                    LT = L_ // P
                    ccm = sb(st, 'fcc', (P, GC, GW), BF16); scm = sb(st, 'fsc', (P, GC, GW), BF16)
                    dma('pool', ccm[:], ins['k_dftc%d' % GW].rearrange("(c p) n -> p c n", p=P), [], ['fcc'])
                    dma('pool', scm[:], ins['k_dfts%d' % GW].rearrange("(c p) n -> p c n", p=P), [], ['fsc'])
                    ut = sb(st, 'fut', (P, GC, L_), BF16)
                    A_ = sb(st, 'fA', (P, LT, GW), BF16); Bn = sb(st, 'fB', (P, LT, GW), BF16)
                    cl = [sb(st, 'fcl%d' % i, (P, LT, 256), BF16) for i in range(2)]
                    sl = [sb(st, 'fsl%d' % i, (P, LT, 256), BF16) for i in range(2)]
                    fo = [sb(st, 'ffo%d' % i, (P, 256), BF16) for i in range(2)]
                    clv = ins['k_dftc%d' % L_].rearrange("(c p) n -> p c n", p=P); slv = ins['k_dfts%d' % L_].rearrange("(c p) n -> p c n", p=P)
                    nt_ = 0; nf = 0
                    for b in range(B):
                        for g in range(4):
                            s0 = b * TB + o_
                            dma('sp', ut[:], uT[g * GW:(g + 1) * GW, s0:s0 + L_].rearrange("(c p) t -> p c t", p=P), [], ['fut'])
                            for pt in range(LT):
                                for c in range(GC):
                                    mm(ps[0][:, :GW], ut[:, c, pt * P:(pt + 1) * P], ccm[:, c, :], c == 0, c == GC - 1, ['fut', 'fcc'], ['ps0'])
                                for c in range(GC):
                                    mm(ps[1][:, :GW], ut[:, c, pt * P:(pt + 1) * P], scm[:, c, :], c == 0, c == GC - 1, ['fut', 'fsc'], ['ps1'])
                                cp(A_[:, pt, :], ps[0][:, :GW], ['ps0'], ['fA'], eng='act')
                                ts(Bn[:, pt, :], ps[1][:, :GW], -1.0, None, ALU.mult, None, ['ps1'], ['fB'])
                            for k0 in range(0, L_, 256):
                                kn_ = min(256, L_ - k0)
                                ti = nt_ % 2; nt_ += 1
                                dma('pool', cl[ti][:, :, :kn_], clv[:, :, k0:k0 + kn_], [], ['fcl%d' % ti])
                                dma('pool', sl[ti][:, :, :kn_], slv[:, :, k0:k0 + kn_], [], ['fsl%d' % ti])
                                for cb in range(GC):
                                    bi = 2 + cb % 2
                                    for pt in range(LT):
                                        mm(ps[bi][:, :kn_], A_[:, pt, cb * P:(cb + 1) * P], cl[ti][:, pt, :kn_], pt == 0, False, ['fA', 'fcl%d' % ti], [PSK[bi]])
                                        mm(ps[bi][:, :kn_], Bn[:, pt, cb * P:(cb + 1) * P], sl[ti][:, pt, :kn_], False, pt == LT - 1, ['fB', 'fsl%d' % ti], [PSK[bi]])
                                    fi = nf % 2; nf += 1
                                    cp(fo[fi][:, :kn_], ps[bi][:, :kn_], [PSK[bi]], ['ffo%d' % fi])
                                    dma('sp', ofT[g * GW + cb * P:g * GW + (cb + 1) * P, s0 + k0:s0 + k0 + kn_], fo[fi][:, :kn_], ['ffo%d' % fi], [])
                    S.barrier()
            if cfg.get('stop_after') == 'attn':
                break
            with ExitStack() as st:
                NT_ = TB // P
                cn_ = {}
                for nm in ('penL', 'penU', 'strictL', 'strictU', 'tri0', 'tri1', 'sel0', 'sel1'):
                    cn_[nm] = sb(st, 'dc_' + nm, (P, P), F32)
                    dma('sp', cn_[nm][:], ins['k_' + nm][:, :], [], ['dconst'])
                pick = sb(st, 'dc_pick', (P, 4, P), F32)
                dma('sp', pick[:], ins['k_pick'].rearrange("n p q -> p n q"), [], ['dconst'])
                rmask = sb(st, 'dc_rmask', (P, 2), F32)
                dma('sp', rmask[:], ins['k_rowmask'][:, :], [], ['dconst'])
                NJ = 3 * DNW // P
                cwT = sb(st, 'dcw', (P, 5, NJ), F32)
                for j in range(5):
                    load_vec_T(st, cwT[:, j, :], 'dcw', dn_conv[li, j, :], 3 * DNW, 'tdc%d' % j)
                npr = 128 + 4 * DH
                prm = sb(st, 'dprm', (1, npr), F32); pbc = sb(st, 'dpb', (P, npr), F32)
                dma('sp', prm[0:1, 0:128], dn_norm_g[li:li + 1, :], [], ['dprm'])
                dma('sp', prm[0:1, 128:128 + 2 * DH], dn_a_log[li:li + 1, :], [], ['dprm'])
                dma('sp', prm[0:1, 128 + 2 * DH:npr], dn_dt_bias[li:li + 1, :], [], ['dprm'])
                mm(ps[7][:, :npr], ones[0:1, :], prm[0:1, :], True, True, ['ones', 'dprm'], ['ps7'])
                cp(pbc[:], ps[7][:, :npr], ['ps7'], ['dpb'])
                gnb = pbc[:, 0:128]
                nea = sb(st, 'dnea', (P, 2 * DH), F32)
                act(nea[:], pbc[:, 128:128 + 2 * DH], AF.Exp, ['dpb'], ['dnea'])
                ts(nea[:], nea[:], -1.0, None, ALU.mult, None, ['dnea'], ['dnea'])
                raw = sb(st, 'draw', (P, TB), F32)
                fm = [sb(st, 'dfm%d' % i, (P, TB), F32) for i in range(3)]
                tmj = [sb(st, 'dtm%d' % i, (P, NT_, P), F32) for i in range(3)]
                sqt = sb(st, 'dsq', (P, 512), F32); rn = sb(st, 'drn', (P, 512), F32)
                ab = sb(st, 'dab', (P, NT_, 4 * DH), F32)
                sm = {}
                for nm in ('t1', 't2'):
                    sm[nm] = sb(st, 'd_' + nm, (P, NT_), F32)
                for d in range(2):
                    for nm in ('G', 'beta', 'gc', 'ekt', 'egc', 'begc', 'qsc', 'nbeta', 'eglA', 'eglB'):
                        sm[nm, d] = sb(st, 'd_%s%d' % (nm, d), (P, NT_), F32)
                Sst = [[sb(st, 'dS%d_%d' % (d, i), (P, P), F32) for i in range(2)] for d in range(2)]
                oacc = [sb(st, 'doacc0', (P, NT_, P), F32)[:], raw[:].rearrange("p (m d) -> p m d", d=P)]
                okey = ['doacc0', 'draw']
                wk_ = {}
                for d in range(2):
                    for nm in ('dg', 'X1', 'gam', 'gamT', 'N0', 'NT0', 'N1', 'NT1', 'qkg', 'wT0', 'wT1', 'qd', 'qdT0', 'qdT1', 'kt', 'vn0', 'vn1', 'u0', 'u1', 'kblk', 'qblk'):
                        wk_[nm, d] = sb(st, 'dw_%s%d' % (nm, d), (P, P), F32)
                    for nm in ('wT0', 'wT1', 'qdT0', 'qdT1'):
                        S.emit('dve', lambda e: e.memset(wk_[nm, d][:], 0.0), [], ['dw_%s%d' % (nm, d)])
                    for nm in ('Xa', 'Xb'):
                        wk_[nm, d] = sb(st, 'dw_%s%d' % (nm, d), (P, 2 * P), F32)
                obo = sb(st, 'dobo', (P, TB), BF16)
                nctx = CL // P
                order = [list(range(NT_)), list(range(nctx - 1, -1, -1)) + list(range(NT_ - 1, nctx - 1, -1))]

                def PSL(bank, slot, w=1):
                    return ps[bank][:, slot * P:(slot + w) * P], 'ps%d' % bank
                for h in range(DH):
                    for b in range(B):
                        bs = b * TB
                        for wi in range(3):
                            row0 = wi * DNW + h * P; jj = row0 // P
                            dst = fm[wi]; dk = 'dfm%d' % wi
                            dma('sp', raw[:], qkvT[row0:row0 + P, bs:bs + TB], [], ['draw'])
                            for (o, ln) in ((0, CL), (CL, SL)):
                                ts(dst[:, o:o + ln], raw[:, o:o + ln], cwT[:, 2, jj:jj + 1], None, ALU.mult, None, ['draw', 'dcw'], [dk])
                                for tap, sh in ((0, -2), (1, -1), (3, 1), (4, 2)):
                                    if sh < 0:
                                        stt(dst[:, o - sh:o + ln], raw[:, o:o + ln + sh], cwT[:, tap, jj:jj + 1], dst[:, o - sh:o + ln], ALU.mult, ALU.add, ['draw', 'dcw', dk], [dk])
                                    else:
                                        stt(dst[:, o:o + ln - sh], raw[:, o + sh:o + ln], cwT[:, tap, jj:jj + 1], dst[:, o:o + ln - sh], ALU.mult, ALU.add, ['draw', 'dcw', dk], [dk])
                            act(dst[:], dst[:], AF.Silu, [dk], [dk])
                            if wi < 2:
                                for t0 in range(0, TB, 512):
                                    tn = min(512, TB - t0)
                                    act(sqt[:, :tn], dst[:, t0:t0 + tn], AF.Square, [dk], ['dsq'])
                                    mm(ps[5][:, :tn], ones[:, :], sqt[:, :tn], True, True, ['ones', 'dsq'], ['ps5'])
                                    act(rn[:, :tn], ps[5][:, :tn], AF.Sqrt, ['ps5'], ['drn'], bias=EPS, scale=1.0)
                                    S.emit('dve', lambda e: e.reciprocal(rn[:, :tn], rn[:, :tn]), ['drn'], ['drn'])
                                    tt(dst[:, t0:t0 + tn], dst[:, t0:t0 + tn], rn[:, :tn], ALU.mult, [dk, 'drn'], [dk])
                            for m in range(NT_):
                                bi = 5 + m % 3
                                tr(ps[bi][:, :P], dst[:, m * P:(m + 1) * P], ident[:, :], [dk, 'ident'], [PSK[bi]])
                                cp(tmj[wi][:, m, :], ps[bi][:, :P], [PSK[bi]], ['dtm%d' % wi], eng=('dve' if m % 2 else 'act'))
                        qTm, kTm = fm[0], fm[1]
                        qtm, ktm, vtm = tmj
                        dma('sp', ab[:], abtm[bs:bs + TB, :].rearrange("(m p) c -> p m c", p=P), [], ['dab'])
                        for d in range(2):
                            ca = d * DH + h; cb_ = 2 * DH + d * DH + h
                            t1 = sm['t1']; t2 = sm['t2']
                            act(t1[:], ab[:, :, ca], AF.Exp, ['dab', 'dpb'], ['d_t1'], bias=pbc[:, 128 + 2 * DH + ca:128 + 2 * DH + ca + 1])
                            act(t1[:], t1[:], AF.Ln, ['d_t1'], ['d_t1'], bias=1.0)
                            ts(sm['G', d][:], t1[:], nea[:, ca:ca + 1], None, ALU.mult, None, ['d_t1', 'dnea'], ['d_G%d' % d])
                            act(sm['beta', d][:], ab[:, :, cb_], AF.Sigmoid, ['dab'], ['d_beta%d' % d])
                            mm(ps[5][:, :NT_], cn_['tri%d' % d][:, :], sm['G', d][:], True, True, ['dconst', 'd_G%d' % d], ['ps5'])
                            cp(sm['gc', d][:], ps[5][:, :NT_], ['ps5'], ['d_gc%d' % d])
                            mm(ps[6][:, :NT_], cn_['sel%d' % d][:, :], sm['gc', d][:], True, True, ['dconst', 'd_gc%d' % d], ['ps6'])
                            tt(t2[:], ps[6][:, :NT_], sm['gc', d][:], ALU.subtract, ['ps6', 'd_gc%d' % d], ['d_t2'])
                            act(sm['ekt', d][:], t2[:], AF.Exp, ['d_t2'], ['d_ekt%d' % d])
                            act(sm['egc', d][:], sm['gc', d][:], AF.Exp, ['d_gc%d' % d], ['d_egc%d' % d])
                            tt(sm['begc', d][:], sm['beta', d][:], sm['egc', d][:], ALU.mult, ['d_beta%d' % d, 'd_egc%d' % d], ['d_begc%d' % d])
                            ts(sm['qsc', d][:], sm['egc', d][:], 128.0 ** -0.5, None, ALU.mult, None, ['d_egc%d' % d], ['d_qsc%d' % d])
                            ts(sm['nbeta', d][:], sm['beta', d][:], -1.0, None, ALU.mult, None, ['d_beta%d' % d], ['d_nbeta%d' % d])
                            for nm, pi in (('eglA', 0 if d == 0 else 2), ('eglB', 1 if d == 0 else 3)):
                                mm(ps[7][:, :NT_], pick[:, pi, :], sm['gc', d][:], True, True, ['dconst', 'd_gc%d' % d], ['ps7'])
                                act(sm[nm, d][:], ps[7][:, :NT_], AF.Exp, ['ps7'], ['d_%s%d' % (nm, d)])
                            S.emit('dve', lambda e: e.memset(Sst[d][0][:], 0.0), [], ['dS%d_0' % d])
                        cur = [0, 0]
                        for step in range(0 if cfg.get('dn_skip_tiles') else NT_):
                            for d in range(2):
                                m = order[d][step]
                                W = lambda nm: wk_[nm, d]
                                K_ = lambda nm: 'dw_%s%d' % (nm, d)
                                pen, penT, strict = (cn_['penL'], cn_['penU'], cn_['strictL']) if d == 0 else (cn_['penU'], cn_['penL'], cn_['strictU'])
                                gcm = sm['gc', d][:, m:m + 1]; gck = 'd_gc%d' % d
                                msl = slice(m * P, (m + 1) * P)
                                pA, kA = PSL(0, 0); pB, kB = PSL(1, 0); pD, kD = PSL(1, 1); pC, kC = PSL(2, 0); pC2, kC2 = PSL(2, 1)
                                pE, kE = PSL(3, 0, 2); pF, kF = PSL(4, 0); pG, kG = PSL(4, 1)
                                pH, kH = PSL(5, 0); pS, kS = PSL(5, 1)
                                pO, kO = PSL(6 + d, 0)
                                if cfg.get('dn_cut0', 99) > 0:
                                    ts(W('dg')[:], ident[:, :], gcm, None, ALU.mult, None, ['ident', gck], [K_('dg')])
                                if cfg.get('dn_cut0', 99) > 1:
                                    mm(pA, ones[:, :], W('dg')[:], True, True, ['ones', K_('dg')], [kA])
                                if cfg.get('dn_cut0', 99) > 2:
                                    stt(W('X1')[:], pA, gcm, pen[:, :], ALU.subtract, ALU.add, [kA, gck, 'dconst'], [K_('X1')])
                                if cfg.get('dn_cut0', 99) > 3:
                                    act(W('gam')[:], W('X1')[:], AF.Exp, [K_('X1')], [K_('gam')], scale=-1.0)
                                if cfg.get('dn_cut0', 99) > 4:
                                    stt(W('X1')[:], pA, gcm, penT[:, :], ALU.subtract, ALU.subtract, [kA, gck, 'dconst'], [K_('X1')])
                                if cfg.get('dn_cut0', 99) > 5:
                                    act(W('gamT')[:], W('X1')[:], AF.Exp, [K_('X1')], [K_('gamT')])
                                if cfg.get('dn_cut', 99) < 1:
                                    continue
                                if cfg.get('dn_cut1', 99) > 0:
                                    cp(W('kblk')[:], kTm[:, msl], ['dfm1'], [K_('kblk')])
                                    mm(pB, W('kblk')[:], kTm[:, msl], True, True, [K_('kblk'), 'dfm1'], [kB])
                                if cfg.get('dn_cut1', 99) > 1:
                                    stt(W('N0')[:], pB, sm['nbeta', d][:, m:m + 1], W('gam')[:], ALU.mult, ALU.mult, [kB, 'd_nbeta%d' % d, K_('gam')], [K_('N0')])
                                if cfg.get('dn_cut1', 99) > 2:
                                    tt(W('N0')[:], W('N0')[:], strict[:, :], ALU.mult, [K_('N0'), 'dconst'], [K_('N0')])
                                if cfg.get('dn_cut1', 99) > 3:
                                    tr(pC, W('N0')[:], ident[:, :], [K_('N0'), 'ident'], [kC])
                                if cfg.get('dn_cut1', 99) > 4:
                                    cp(W('NT0')[:], pC, [kC], [K_('NT0')])
                                if cfg.get('dn_cut', 99) < 2:
                                    continue
                                mm(pD, W('kblk')[:], qTm[:, msl], True, True, [K_('kblk'), 'dfm0'], [kD])
                                stt(W('qkg')[:], pD, 128.0 ** -0.5, W('gamT')[:], ALU.mult, ALU.mult, [kD, K_('gamT')], [K_('qkg')])
                                ts(W('Xa')[:, 0:P], vtm[:, m, :], sm['beta', d][:, m:m + 1], None, ALU.mult, None, ['dtm2', 'd_beta%d' % d], [K_('Xa')])
                                ts(W('Xa')[:, P:2 * P], ktm[:, m, :], sm['begc', d][:, m:m + 1], None, ALU.mult, None, ['dtm1', 'd_begc%d' % d], [K_('Xa')])
                                if cfg.get('dn_cut', 99) < 3:
                                    continue
                                Nc, NTc, Nn, NTn = 'N0', 'NT0', 'N1', 'NT1'
                                Xc, Xn = 'Xa', 'Xb'
                                for lvl in range(6):
                                    mm(pE, W(NTc)[:], W(Xc)[:], True, True, [K_(NTc), K_(Xc)], [kE])
                                    tt(W(Xn)[:], W(Xc)[:], pE, ALU.add, [K_(Xc), kE], [K_(Xn)])
                                    Xc, Xn = Xn, Xc
                                    if lvl < 5:
                                        mm(pF, W(NTc)[:], W(Nc)[:], True, True, [K_(NTc), K_(Nc)], [kF])
                                        mm(pG, W(Nc)[:], W(NTc)[:], True, True, [K_(NTc), K_(Nc)], [kG])
                                        cp(W(Nn)[:], pF, [kF], [K_(Nn)], eng='act')
                                        cp(W(NTn)[:], pG, [kG], [K_(NTn)])
                                        Nc, Nn = Nn, Nc; NTc, NTn = NTn, NTc
                                if cfg.get('dn_cut', 99) < 4:
                                    continue
                                X = W(Xc); Xk = K_(Xc)
                                tr(pC, X[:, P:2 * P], ident[:, :], [Xk, 'ident'], [kC])
                                cp(W('wT0')[:, 0:64], pC[:, 0:64], [kC], [K_('wT0')], eng='act')
                                cp(W('wT1')[:, 64:P], pC[:, 64:P], [kC], [K_('wT1')], eng='act')
                                ts(W('qd')[:], qtm[:, m, :], sm['qsc', d][:, m:m + 1], None, ALU.mult, None, ['dtm0', 'd_qsc%d' % d], [K_('qd')])
                                tr(pC2, W('qd')[:], ident[:, :], [K_('qd'), 'ident'], [kC2])
                                cp(W('qdT0')[:, 0:64], pC2[:, 0:64], [kC2], [K_('qdT0')])
                                cp(W('qdT1')[:, 64:P], pC2[:, 64:P], [kC2], [K_('qdT1')])
                                ts(W('kt')[:], ktm[:, m, :], sm['ekt', d][:, m:m + 1], None, ALU.mult, None, ['dtm1', 'd_ekt%d' % d], [K_('kt')])
                                if cfg.get('dn_cut', 99) < 5:
                                    continue
                                for c_ in range(2):
                                    ts(W('u%d' % c_)[:], X[:, 0:P], rmask[:, c_:c_ + 1], None, ALU.mult, None, [Xk, 'dconst'], [K_('u%d' % c_)])
                                for ci_, c_ in enumerate((0, 1) if d == 0 else (1, 0)):
                                    So = Sst[d][cur[d]]; Sn = Sst[d][1 - cur[d]]
                                    Sok = 'dS%d_%d' % (d, cur[d]); Snk = 'dS%d_%d' % (d, 1 - cur[d])
                                    egl = sm['eglA' if c_ == 0 else 'eglB', d]; eglk = 'd_%s%d' % ('eglA' if c_ == 0 else 'eglB', d)
                                    mm(pH, W('wT%d' % c_)[:], So[:, :], True, True, [K_('wT%d' % c_), Sok], [kH])
                                    mm(pO, W('qdT%d' % c_)[:], So[:, :], ci_ == 0, False, [K_('qdT%d' % c_), Sok], [kO])
                                    tt(W('vn%d' % c_)[:], W('u%d' % c_)[:], pH, ALU.subtract, [K_('u%d' % c_), kH], [K_('vn%d' % c_)])
                                    mm(pS, W('kt')[:], W('vn%d' % c_)[:], True, True, [K_('kt'), K_('vn%d' % c_)], [kS])
                                    stt(Sn[:, :], So[:, :], egl[:, m:m + 1], pS, ALU.mult, ALU.add, [Sok, eglk, kS], [Snk])
                                    cur[d] = 1 - cur[d]
                                if cfg.get('dn_cut', 99) < 6:
                                    continue
                                tt(W('vn0')[:], W('vn0')[:], W('vn1')[:], ALU.add, [K_('vn0'), K_('vn1')], [K_('vn0')])
                                mm(pO, W('qkg')[:], W('vn0')[:], False, True, [K_('qkg'), K_('vn0')], [kO])
                                cp(oacc[d][:, m, :], pO, [kO], [okey[d]], eng='act')
                        oa_ = oacc[0].rearrange("p m d -> p (m d)"); ob_ = raw[:]
                        tt(oa_, oa_, ob_, ALU.add, ['doacc0', 'draw'], ['doacc0'])
                        tt(raw[:], oa_, oa_, ALU.mult, ['doacc0'], ['draw'])
                        S.emit('dve', lambda e: e.tensor_reduce(sm['t1'][:], raw[:].rearrange("p (m d) -> p m d", d=P), AX.X, ALU.add), ['draw'], ['d_t1'])
                        act(sm['t1'][:], sm['t1'][:], AF.Sqrt, ['d_t1'], ['d_t1'], bias=EPS, scale=1.0 / 128)
                        S.emit('dve', lambda e: e.reciprocal(sm['t1'][:], sm['t1'][:]), ['d_t1'], ['d_t1'])
                        zt = fm[2][:].rearrange("p (m d) -> p m d", d=P)
                        dma('sp', zt, ztm[bs:bs + TB, h * P:(h + 1) * P].rearrange("(m p) d -> p m d", p=P), [], ['dfm2'])
                        act(fm[2][:], fm[2][:], AF.Silu, ['dfm2'], ['dfm2'])
                        for m in range(NT_):
                            stt(oacc[0][:, m, :], oacc[0][:, m, :], sm['t1'][:, m:m + 1], gnb, ALU.mult, ALU.mult, ['doacc0', 'd_t1', 'dpb'], ['doacc0'])
                        tt(oa_, oa_, fm[2][:], ALU.mult, ['doacc0', 'dfm2'], ['doacc0'])
                        for m in range(NT_):
                            bi = 5 + m % 3
                            tr(ps[bi][:, :P], oacc[0][:, m, :], ident[:, :], ['doacc0', 'ident'], [PSK[bi]])
                            cp(obo[:, m * P:(m + 1) * P], ps[bi][:, :P], [PSK[bi]], ['dobo'], eng=('dve' if m % 2 else 'act'))
                        dma('sp', obT[h * P:(h + 1) * P, bs:bs + TB], obo[:], ['dobo'], [])
                S.barrier()
            if cfg.get('stop_after') == 'dn':
                break
            with ExitStack() as st:
                KA, KB_, KF, KG = H * 128 // P, DNW // P, FNW // P, GR // P
                bgT = sb(st, 'bgT', (P, 3 * DC), F32)
                load_vec_T(st, bgT, 'bgT', b_gate[li, :], 3 * D, 'tbg')
                srcs = ((oaT, KA, w_br_a[li]), (obT, KB_, w_br_b[li]), (ofT, KF, w_br_c[li]))
                wts = [[sb(st, 'mw%d_%d' % (k, i), (P, kk, 512), BF16) for i in range(2)] for k, (_, kk, _) in enumerate(srcs)]
                wg = [sb(st, 'mwg%d' % i, (P, KG, 3, 512), BF16) for i in range(2)]
                acs = [[sb(st, 'ma%d_%d' % (k, i), (P, kk, 512), BF16) for i in range(2)] for k, (_, kk, _) in enumerate(srcs)]
                ag = [sb(st, 'mag%d' % i, (P, KG, 512), BF16) for i in range(2)]
                gsb_ = sb(st, 'mgs', (P, 512), F32); yacc = sb(st, 'myacc', (P, 512), F32); ytmp = sb(st, 'mytmp', (P, 512), F32)
                yo = [sb(st, 'myo%d' % i, (P, 512), BF16) for i in range(2)]
                wgv = w_gate_up[li].rearrange("(p j) n -> p j n", j=KG)
                glv = glrT.rearrange("(p j) t -> p j t", j=KG)
                ai = 0; ny = 0
                for ci, c0 in enumerate(range(0, D, 512)):
                    wi = ci % 2
                    for k, (_, kk, W_) in enumerate(srcs):
                        dma('pool', wts[k][wi][:], W_.rearrange("(p j) n -> p j n", j=kk)[:, :, c0:c0 + 512], [], ['mw%d_%d' % (k, wi)])
                    for br in range(3):
                        dma('pool', wg[wi][:, :, br, :], wgv[:, :, br * D + c0:br * D + c0 + 512], [], ['mwg%d' % wi])
                    for (t0, tn, seg) in blk:
                        aj = ai % 2; ai += 1
                        for k, (A_T, kk, _) in enumerate(srcs):
                            dma('sp', acs[k][aj][:, :, :tn], A_T.rearrange("(p j) t -> p j t", j=kk)[:, :, t0:t0 + tn], [], ['ma%d_%d' % (k, aj)])
                        dma('sp', ag[aj][:, :, :tn], glv[:, :, t0:t0 + tn], [], ['mag%d' % aj])
                        for s0 in range(0, 512, P):
                            jcol = (c0 + s0) // P
                            for br, (_, kk, _) in enumerate(srcs):
                                pb = ps[(br % 2) * 2]; pbk = PSK[(br % 2) * 2]; pg = ps[(br % 2) * 2 + 1]; pgk = PSK[(br % 2) * 2 + 1]
                                for j in range(kk):
                                    mm(pb[:, :tn], wts[br][wi][:, j, s0:s0 + P], acs[br][aj][:, j, :tn], j == 0, j == kk - 1,
                                       ['mw%d_%d' % (br, wi), 'ma%d_%d' % (br, aj)], [pbk])
                                for j in range(KG):
                                    mm(pg[:, :tn], wg[wi][:, j, br, s0:s0 + P], ag[aj][:, j, :tn], j == 0, j == KG - 1, ['mwg%d' % wi, 'mag%d' % aj], [pgk])
                                act(gsb_[:, :tn], pg[:, :tn], AF.Sigmoid, [pgk, 'bgT'], ['mgs'], bias=bgT[:, br * DC + jcol:br * DC + jcol + 1])
                                if br == 0:
                                    tt(yacc[:, :tn], pb[:, :tn], gsb_[:, :tn], ALU.mult, [pbk, 'mgs'], ['myacc'])
                                else:
                                    tt(ytmp[:, :tn], pb[:, :tn], gsb_[:, :tn], ALU.mult, [pbk, 'mgs'], ['mytmp'])
                                    tt(yacc[:, :tn], yacc[:, :tn], ytmp[:, :tn], ALU.add, ['myacc', 'mytmp'], ['myacc'])
                            yi = ny % 2; ny += 1
                            cp(yo[yi][:, :tn], yacc[:, :tn], ['myacc'], ['myo%d' % yi], eng='act')
                            dma('sp', yT[c0 + s0:c0 + s0 + P, t0:t0 + tn], yo[yi][:, :tn], ['myo%d' % yi], [])
                S.barrier()

            def xupd_handler(st, name, gt_ap):
                xb = [sb(st, '%s_xb%d' % (name, i), (P, 512), F32) for i in range(3)]
                cnt = [0]

                def h(c0, m, t0, tn, seg, pst, pk):
                    i = cnt[0] % 3; cnt[0] += 1
                    k = '%s_xb%d' % (name, i)
                    dma('sp', xb[i][:m, :tn], xT[c0:c0 + m, t0:t0 + tn], [], [k])
                    stt(xb[i][:m, :tn], pst[:m, :tn], gt_ap[:m, c0 // P, seg:seg + 1], xb[i][:m, :tn], ALU.mult, ALU.add, [pk, k, 'modT'], [k])
                    dma('sp', xT[c0:c0 + m, t0:t0 + tn], xb[i][:m, :tn], [k], [])
                return h

            with ExitStack() as st:
                gemm(st, 'g6', yT, D, w_out[li], D, blk, xupd_handler(st, 'g6', gt1))
                S.barrier()

            norm_stage('n4', xT, hT, D, gsT[:, 1, :, :], sh2)
            with ExitStack() as st:
                gemm(st, 'g7', hT, D, w_ffn_up[li], 2 * DFF, blk, store_handler(st, 'g7', upT, BF16))
                S.barrier()
            with ExitStack() as st:
                NFC = 2 * DFF // P
                cw = sb(st, 'fcw', (P, 3, NFC), F32)
                for j in range(3):
                    load_vec_T(st, cw[:, j, :], 'fcw', ffn_conv[li, j, :], 2 * DFF, 'tfc%d' % j)
                LM = max(CL, SL)
                raw = [[sb(st, 'fr%d_%d' % (w_, i), (P, LM), BF16) for i in range(2)] for w_ in range(2)]
                yv = [sb(st, 'fy%d' % w_, (P, LM), F32) for w_ in range(2)]
                mo = [sb(st, 'fmo%d' % i, (P, LM), BF16) for i in range(2)]
                n = 0
                for cch in range(DFF // P):
                    for (s0, ln, seg) in segs:
                        i = n % 2; n += 1
                        for w_ in range(2):
                            r0 = w_ * DFF + cch * P; jj = r0 // P
                            rk = 'fr%d_%d' % (w_, i); yk = 'fy%d' % w_
                            x_ = raw[w_][i]; y_ = yv[w_]
                            dma('sp', x_[:, :ln], upT[r0:r0 + P, s0:s0 + ln], [], [rk])
                            ts(y_[:, :ln], x_[:, :ln], cw[:, 1, jj:jj + 1], None, ALU.mult, None, [rk, 'fcw'], [yk])
                            stt(y_[:, 1:ln], x_[:, 0:ln - 1], cw[:, 0, jj:jj + 1], y_[:, 1:ln], ALU.mult, ALU.add, [rk, 'fcw', yk], [yk])
                            stt(y_[:, 0:ln - 1], x_[:, 1:ln], cw[:, 2, jj:jj + 1], y_[:, 0:ln - 1], ALU.mult, ALU.add, [rk, 'fcw', yk], [yk])
                        act(yv[0][:, :ln], yv[0][:, :ln], AF.Silu, ['fy0'], ['fy0'])
                        tt(mo[i][:, :ln], yv[0][:, :ln], yv[1][:, :ln], ALU.mult, ['fy0', 'fy1'], ['fmo%d' % i])
                        dma('sp', mT[cch * P:(cch + 1) * P, s0:s0 + ln], mo[i][:, :ln], ['fmo%d' % i], [])
                S.barrier()
            with ExitStack() as st:
                gemm(st, 'g8', mT, DFF, w_ffn_down[li], D, blk, xupd_handler(st, 'g8', gt2))
                S.barrier()

        with ExitStack() as st:
            gf = sb(st, 'gf', (P, DC), F32)
            load_vec_T(st, gf, 'gf', g_final[0, :], D, 'tgf')
            xs = [sb(st, 'fx%d' % i, (P, DC, P), F32) for i in range(2)]
            sq = sb(st, 'fsq', (P, DC, P), F32)
            rs = sb(st, 'frs', (P, P), F32)
            ot = [sb(st, 'fo%d' % i, (P, D), F32) for i in range(2)]
            xv = xT.rearrange("(c p) t -> p c t", p=P)
            n = 0
            for b in range(B):
                for r0 in range(0, SL, P):
                    i = n % 2; n += 1
                    t0 = b * TB + CL + r0
                    xk = 'fx%d' % i; ok = 'fo%d' % i
                    dma('sp', xs[i][:], xv[:, :, t0:t0 + P], [], [xk])
                    act(sq[:], xs[i][:], AF.Square, [xk], ['fsq'])
                    for c in range(DC):
                        mm(ps[4][:, :P], ones[:, :], sq[:, c, :], c == 0, c == DC - 1, ['ones', 'fsq'], ['ps4'])
                    act(rs[:], ps[4][:, :P], AF.Sqrt, ['ps4'], ['frs'], bias=EPS, scale=1.0 / D)
                    S.emit('dve', lambda e: e.reciprocal(rs[:], rs[:]), ['frs'], ['frs'])
                    for c in range(DC):
                        stt(xs[i][:, c, :], xs[i][:, c, :], gf[:, c:c + 1], rs[:], ALU.mult, ALU.mult, [xk, 'gf', 'frs'], [xk])
                    for c4 in range(0, DC, 4):
                        bi = (c4 // 4) % 4
                        nn = min(DC, c4 + 4) - c4
                        for c in range(c4, c4 + nn):
                            tr(ps[bi][:, (c - c4) * P:(c - c4 + 1) * P], xs[i][:, c, :], ident[:, :], [xk, 'ident'], [PSK[bi]])
                        cp(ot[i][:, c4 * P:(c4 + nn) * P], ps[bi][:, :nn * P], [PSK[bi]], [ok], eng=('dve' if (c4 // 4) % 2 else 'act'))
                    dma('sp', out_ap[b, r0:r0 + P, :], ot[i][:, :], [ok], ['OUT'])
            S.barrier()
        print('instructions:', S.ninst)
    return nc, ins


def make_inmap(cfg, inputs, b=None):
    m = {}
    for k, v in inputs.items():
        a = np.asarray(v, dtype=np.float32)
        if b is not None and k in ('x', 'c', 'ctx'):
            a = a[b:b + 1]
        a = np.ascontiguousarray(a)
        if k in ('c_ctx', 'g_final'):
            a = a.reshape(1, -1)
        if k in ('dn_a_log', 'dn_dt_bias'):
            a = a.reshape(a.shape[0], -1)
        m[k] = a
    for n, v in host_consts(cfg).items():
        m['k_' + n] = np.ascontiguousarray(v)
    return m


def kernel(**inputs):
    nb = int(np.asarray(inputs['x']).shape[0])
    cfg = make_cfg(B=1)
    nc, _ = build(cfg)
    shared = make_inmap(cfg, inputs, 0)
    ims = [shared]
    for b in range(1, nb):
        m = dict(shared)
        for k in ('x', 'c', 'ctx'):
            m[k] = np.ascontiguousarray(np.asarray(inputs[k], dtype=np.float32)[b:b + 1])
        ims.append(m)
    res = run_bass_kernel_spmd(nc, ims, core_ids=list(range(nb)))
    return np.concatenate([np.asarray(r['out']) for r in res.results], axis=0).astype(np.float32)
```
